# Optimizing a Trainium2 kernel written in Bass

```python
import jax, jax.numpy as jnp
from jax import lax
import numpy as np


D_MODEL = 4096
BATCH = 4
SEQ = 4096
DEPTH = 4

HEAD_DIM = 128
D_MIX = D_MODEL
N_MIXERS = 4
GROUP_HEADS = D_MIX // (N_MIXERS * HEAD_DIM)
GROUP_WIDTH = GROUP_HEADS * HEAD_DIM
RET_HEADS = GROUP_HEADS
RET_CHUNK = 128
RET_ROT_BASE = 10000.0
SB_HEADS = GROUP_HEADS
SB_Q_BLOCK = 128
LRU_WIDTH = GROUP_WIDTH
LRU_BLOCKS = GROUP_HEADS
LRU_BLOCK_W = LRU_WIDTH // LRU_BLOCKS
LRU_CONV = 4
LRU_C = 8.0
NSA_HEADS = GROUP_HEADS
NSA_KV_HEADS = 2
NSA_KV_WIDTH = NSA_KV_HEADS * HEAD_DIM
CMP_LEN = 32
CMP_STRIDE = 16
CMP_HIDDEN = 256
SLC_LEN = 64
SLC_TOPK = 16
SLC_Q_BLOCK = 64
WINDOW = 512
WIN_Q_BLOCK = 128
ROPE_THETA = 500000.0
ROPE_DIMS = HEAD_DIM // 4
D_FF = 256 * ((8 * D_MODEL // 3 + 255) // 256)
MLP_CONV = 3
NORM_EPS = 1e-6
SPLIT_SIZES = tuple(4 * [GROUP_WIDTH] + 3 * [GROUP_WIDTH] + 2 * [LRU_WIDTH] + [GROUP_WIDTH] + 6 * [NSA_KV_WIDTH] + [3 * NSA_HEADS])
N_IN = sum(SPLIT_SIZES)

kernel_name = 'hybrid_parallel_heads_decoder'

F32 = jnp.float32


def rms_norm(x, w):
    xf = x.astype(F32)
    y = xf * lax.rsqrt(jnp.mean(xf * xf, axis=-1, keepdims=True) + NORM_EPS)
    return (y * w.astype(F32)).astype(x.dtype)


def rms_unit(x):
    xf = x.astype(F32)
    return xf * lax.rsqrt(jnp.mean(xf * xf, axis=-1, keepdims=True) + NORM_EPS)


def split_heads(x, n_heads):
    b, s, _ = x.shape
    return x.reshape(b, s, n_heads, -1).transpose(0, 2, 1, 3)


def merge_heads(x):
    b, h, s, d = x.shape
    return x.transpose(0, 2, 1, 3).reshape(b, s, h * d)


def apply_rotary(x, pos, inv_freq):
    half = inv_freq.shape[0]
    ang = pos.astype(F32)[:, None] * inv_freq[None, :]
    cos, sin = jnp.cos(ang), jnp.sin(ang)
    xf = x.astype(F32)
    x1, x2 = xf[..., :half], xf[..., half:2 * half]
    out = jnp.concatenate([x1 * cos - x2 * sin, x2 * cos + x1 * sin, xf[..., 2 * half:]], axis=-1)
    return out.astype(x.dtype)


def partial_rope_freqs():
    half = ROPE_DIMS // 2
    return ROPE_THETA ** (-jnp.arange(half, dtype=F32) / half)


def masked_softmax(s, mask):
    s = jnp.where(mask, s.astype(F32), -jnp.inf)
    m = jnp.max(s, axis=-1, keepdims=True)
    m = jnp.where(jnp.isfinite(m), m, 0.0)
    e = jnp.where(mask, jnp.exp(s - m), 0.0)
    return e / jnp.maximum(jnp.sum(e, axis=-1, keepdims=True), 1e-30)


def causal_dwconv(x, w, b):
    width, s = w.shape[0], x.shape[1]
    xp = jnp.pad(x, ((0, 0), (width - 1, 0), (0, 0)))
    y = b.astype(x.dtype)
    for k in range(width):
        y = y + xp[:, k:k + s] * w[k].astype(x.dtype)
    return y


def retention_chunkwise(q, k, v):
    b, h, s, d = q.shape
    c = RET_CHUNK
    n = s // c
    log_g = jnp.log(1.0 - 2.0 ** (-5.0 - jnp.arange(h, dtype=F32)))
    idx = jnp.arange(c, dtype=F32)
    rel = idx[:, None] - idx[None, :]
    inner_decay = jnp.where(rel >= 0, jnp.exp(log_g[:, None, None] * jnp.maximum(rel, 0.0)), 0.0)
    q_decay = jnp.exp(log_g[:, None] * (idx + 1.0))[..., None]
    k_decay = jnp.exp(log_g[:, None] * (c - 1.0 - idx))[..., None]
    chunk_decay = jnp.exp(log_g * c)[:, None, None]
    to_chunks = lambda t: t.reshape(b, h, n, c, d).transpose(2, 0, 1, 3, 4)

    def step(state, inp):
        qb, kb, vb = inp
        inner = jnp.einsum('bhid,bhjd->bhij', qb, kb) * inner_decay
        o = jnp.einsum('bhij,bhjd->bhid', inner, vb) + jnp.einsum('bhid,bhde->bhie', qb, state) * q_decay
        state = state * chunk_decay + jnp.einsum('bhjd,bhje->bhde', kb * k_decay, vb)
        return state, o

    state0 = jnp.zeros((b, h, d, d), F32)
    _, o = lax.scan(step, state0, (to_chunks(q), to_chunks(k), to_chunks(v)))
    return o.transpose(1, 2, 0, 3, 4).reshape(b, h, s, d)


def retention_mixer(q, k, v, g):
    s = q.shape[1]
    pos = jnp.arange(s)
    inv = 1.0 / (RET_ROT_BASE ** jnp.linspace(0.0, 1.0, HEAD_DIM // 2, dtype=F32))
    qh = apply_rotary(split_heads(q, RET_HEADS), pos, inv).astype(F32)
    kh = apply_rotary(split_heads(k, RET_HEADS), pos, inv).astype(F32) * (HEAD_DIM ** -0.5)
    vh = split_heads(v, RET_HEADS).astype(F32)
    o = rms_unit(retention_chunkwise(qh, kh, vh))
    return merge_heads(o) * jax.nn.silu(g.astype(F32))


def stick_breaking_mixer(q, k, v):
    qh = split_heads(q, SB_HEADS).astype(F32)
    kh = split_heads(k, SB_HEADS).astype(F32)
    vh = split_heads(v, SB_HEADS).astype(F32)
    b, h, s, d = qh.shape
    nb = s // SB_Q_BLOCK
    scale = d ** -0.5
    key_pos = jnp.arange(s)
    q_blocks = qh.reshape(b, h, nb, SB_Q_BLOCK, d).transpose(2, 0, 1, 3, 4)

    def block(args):
        i, qi = args
        qpos = i * SB_Q_BLOCK + jnp.arange(SB_Q_BLOCK)
        z = jnp.einsum('bhqd,bhkd->bhqk', qi, kh) * scale
        past = key_pos[None, :] < qpos[:, None]
        log_beta = jax.nn.log_sigmoid(z)
        log_stay_j = jnp.where(past, jax.nn.log_sigmoid(-z), 0.0)
        log_stay = lax.cumsum(log_stay_j, axis=3, reverse=True) - log_stay_j
        w = jnp.where(past, jnp.exp(log_beta + log_stay), 0.0)
        return jnp.einsum('bhqk,bhkd->bhqd', w, vh)

    o = lax.map(block, (jnp.arange(nb), q_blocks))
    o = o.transpose(1, 2, 0, 3, 4).reshape(b, h, s, d)
    return merge_heads(o)


def rg_lru(x, w_a, b_a, w_x, b_x, lam):
    b, s, c = x.shape
    xb = x.reshape(b, s, LRU_BLOCKS, LRU_BLOCK_W)
    r = jax.nn.sigmoid((jnp.einsum('bsnc,ncd->bsnd', xb, w_a).reshape(b, s, c) + b_a).astype(F32))
    i = jax.nn.sigmoid((jnp.einsum('bsnc,ncd->bsnd', xb, w_x).reshape(b, s, c) + b_x).astype(F32))
    log_a = LRU_C * r * jax.nn.log_sigmoid(lam.astype(F32))
    a = jnp.exp(log_a)
    pos = jnp.arange(s)[:, None]
    mult = jnp.where(pos == 0, 1.0, jnp.sqrt(jnp.maximum(-jnp.expm1(2.0 * log_a), 0.0)))
    u = mult * (i * x.astype(F32))

    def combine(left, right):
        a1, b1 = left
        a2, b2 = right
        return a1 * a2, a2 * b1 + b2

    _, hseq = lax.associative_scan(combine, (a, u), axis=1)
    return hseq


def rglru_mixer(gate_in, rec_in, conv_w, conv_b, w_a, b_a, w_x, b_x, lam):
    xr = causal_dwconv(rec_in, conv_w, conv_b)
    hseq = rg_lru(xr, w_a, b_a, w_x, b_x, lam)
    return hseq * jax.nn.gelu(gate_in.astype(F32))


def nsa_compress(x, pos_emb, w1, w2):
    s = x.shape[2]
    nc = (s - CMP_LEN) // CMP_STRIDE + 1
    idx = np.arange(nc)[:, None] * CMP_STRIDE + np.arange(CMP_LEN)[None, :]
    blocks = x[:, :, idx] + pos_emb
    hid = jax.nn.gelu(jnp.einsum('bgnld,ldh->bgnh', blocks, w1))
    return jnp.einsum('bgnh,hd->bgnd', hid, w2)


def nsa_mixer(q, kc, vc, ks, vs, kw, vw, gate_logits, q_norm_w, k_norm_w, cmp_pos, cmp_w1, cmp_w2):
    b, s, _ = q.shape
    h, g, d = NSA_HEADS, NSA_KV_HEADS, HEAD_DIM
    r = h // g
    scale = d ** -0.5
    pos = jnp.arange(s)
    inv = partial_rope_freqs()
    qg = apply_rotary(rms_norm(split_heads(q, h), q_norm_w), pos, inv).reshape(b, g, r, s, d)

    nc = (s - CMP_LEN) // CMP_STRIDE + 1
    cmp_end = jnp.arange(nc) * CMP_STRIDE + CMP_LEN - 1
    k_cmp = nsa_compress(split_heads(kc, g), cmp_pos[0], cmp_w1[0], cmp_w2[0])
    v_cmp = nsa_compress(split_heads(vc, g), cmp_pos[1], cmp_w1[1], cmp_w2[1])
    k_cmp = apply_rotary(rms_norm(k_cmp, k_norm_w[0]), cmp_end, inv)
    s_cmp = jnp.einsum('bgrtd,bgnd->bgrtn', qg, k_cmp) * scale
    p_cmp = masked_softmax(s_cmp, cmp_end[None, :] <= pos[:, None])
    o_cmp = jnp.einsum('bgrtn,bgnd->bgrtd', p_cmp, v_cmp)

    n_slc = s // SLC_LEN
    ci = np.arange(nc)[:, None] * CMP_STRIDE
    sj = np.arange(n_slc)[None, :] * SLC_LEN
    overlap = np.maximum(0, np.minimum(ci + CMP_LEN, sj + SLC_LEN) - np.maximum(ci, sj)) / CMP_STRIDE
    p_slc = jnp.einsum('bgrtn,nj->bgtj', p_cmp, jnp.asarray(overlap, F32))
    blk = jnp.arange(n_slc)[None, :]
    cur = (pos // SLC_LEN)[:, None]
    valid_blk = blk <= cur
    forced = (blk == 0) | (blk == cur) | (blk == cur - 1)
    sel_score = jnp.where(forced, jnp.inf, jnp.where(valid_blk, p_slc, -jnp.inf))
    k_sel = min(SLC_TOPK, n_slc)
    _, sel_idx = lax.top_k(sel_score, k_sel)

    ksb = apply_rotary(rms_norm(split_heads(ks, g), k_norm_w[1]), pos, inv).reshape(b, g, n_slc, SLC_LEN, d)
    vsb = split_heads(vs, g).reshape(b, g, n_slc, SLC_LEN, d)
    nq = s // SLC_Q_BLOCK
    q_blocks = qg.reshape(b, g, r, nq, SLC_Q_BLOCK, d).transpose(3, 0, 1, 2, 4, 5)
    idx_blocks = sel_idx.reshape(b, g, nq, SLC_Q_BLOCK, k_sel).transpose(2, 0, 1, 3, 4)
    b_ix = jnp.arange(b)[:, None, None, None]
    g_ix = jnp.arange(g)[None, :, None, None]

    def slc_block(args):
        c, qi, ids = args
        tq = c * SLC_Q_BLOCK + jnp.arange(SLC_Q_BLOCK)
        kb = ksb[b_ix, g_ix, ids].reshape(b, g, SLC_Q_BLOCK, k_sel * SLC_LEN, d)
        vb = vsb[b_ix, g_ix, ids].reshape(b, g, SLC_Q_BLOCK, k_sel * SLC_LEN, d)
        tok = ids[..., None] * SLC_LEN + jnp.arange(SLC_LEN)
        mask = (tok <= tq[:, None, None]).reshape(b, g, 1, SLC_Q_BLOCK, k_sel * SLC_LEN)
        sc = jnp.einsum('bgrqd,bgqnd->bgrqn', qi, kb) * scale
        p = masked_softmax(sc, mask)
        return jnp.einsum('bgrqn,bgqnd->bgrqd', p, vb)

    o_slc = lax.map(slc_block, (jnp.arange(nq), q_blocks, idx_blocks))
    o_slc = o_slc.transpose(1, 2, 3, 0, 4, 5).reshape(b, g, r, s, d)

    kwh = apply_rotary(rms_norm(split_heads(kw, g), k_norm_w[2]), pos, inv)
    vwh = split_heads(vw, g)
    pad = ((0, 0), (0, 0), (WINDOW, 0), (0, 0))
    kwp, vwp = jnp.pad(kwh, pad), jnp.pad(vwh, pad)
    nw = s // WIN_Q_BLOCK
    span = WINDOW + WIN_Q_BLOCK
    qw_blocks = qg.reshape(b, g, r, nw, WIN_Q_BLOCK, d).transpose(3, 0, 1, 2, 4, 5)

    def win_block(args):
        c, qi = args
        start = c * WIN_Q_BLOCK
        kb = lax.dynamic_slice_in_dim(kwp, start, span, axis=2)
        vb = lax.dynamic_slice_in_dim(vwp, start, span, axis=2)
        tq = start + jnp.arange(WIN_Q_BLOCK)
        kpos = start - WINDOW + jnp.arange(span)
        dist = tq[:, None] - kpos[None, :]
        mask = (dist >= 0) & (dist < WINDOW) & (kpos[None, :] >= 0)
        sc = jnp.einsum('bgrqd,bgkd->bgrqk', qi, kb) * scale
        p = masked_softmax(sc, mask)
        return jnp.einsum('bgrqk,bgkd->bgrqd', p, vb)

    o_win = lax.map(win_block, (jnp.arange(nw), qw_blocks))
    o_win = o_win.transpose(1, 2, 3, 0, 4, 5).reshape(b, g, r, s, d)

    gates = jax.nn.sigmoid(gate_logits.astype(F32)).reshape(b, s, g, r, 3).transpose(0, 2, 3, 1, 4)
    o = gates[..., 0:1] * o_cmp + gates[..., 1:2] * o_slc + gates[..., 2:3] * o_win
    return merge_heads(o.reshape(b, h, s, d))


def setup_inputs(seed: int = 0) -> dict:
    key = jax.random.key(seed)
    ks = jax.random.split(key, 24)
    L = DEPTH

    def nrm(k, shape, fan_in):
        return jax.random.normal(k, shape, F32) * (fan_in ** -0.5)

    def gain(k, shape):
        return 1.0 + 0.02 * jax.random.normal(k, shape, F32)

    def bias(k, shape):
        return 0.02 * jax.random.normal(k, shape, F32)

    a_c = jax.random.uniform(ks[9], (L, LRU_WIDTH), F32, 0.9, 0.999)
    a = a_c ** (1.0 / LRU_C)
    return {
        'x': jax.random.normal(ks[0], (BATCH, SEQ, D_MODEL), F32),
        'attn_norm_w': gain(ks[1], (L, D_MODEL)),
        'w_in': nrm(ks[2], (L, D_MODEL, N_IN), D_MODEL),
        'lru_conv_w': nrm(ks[3], (L, LRU_CONV, LRU_WIDTH), LRU_CONV),
        'lru_conv_b': bias(ks[4], (L, LRU_WIDTH)),
        'lru_w_a': nrm(ks[5], (L, LRU_BLOCKS, LRU_BLOCK_W, LRU_BLOCK_W), LRU_BLOCK_W),
        'lru_b_a': bias(ks[6], (L, LRU_WIDTH)),
        'lru_w_x': nrm(ks[7], (L, LRU_BLOCKS, LRU_BLOCK_W, LRU_BLOCK_W), LRU_BLOCK_W),
        'lru_b_x': bias(ks[8], (L, LRU_WIDTH)),
        'lru_lambda': jnp.log(a) - jnp.log1p(-a),
        'nsa_q_norm_w': gain(ks[10], (L, HEAD_DIM)),
        'nsa_k_norm_w': gain(ks[11], (L, 3, HEAD_DIM)),
        'nsa_cmp_pos': bias(ks[12], (L, 2, CMP_LEN, HEAD_DIM)),
        'nsa_cmp_w1': nrm(ks[13], (L, 2, CMP_LEN, HEAD_DIM, CMP_HIDDEN), CMP_LEN * HEAD_DIM),
        'nsa_cmp_w2': nrm(ks[14], (L, 2, CMP_HIDDEN, HEAD_DIM), CMP_HIDDEN),
        'w_out': nrm(ks[15], (L, D_MIX, D_MODEL), D_MIX),
        'mlp_norm_w': gain(ks[16], (L, D_MODEL)),
        'w_gate': nrm(ks[17], (L, D_MODEL, D_FF), D_MODEL),
        'w_up': nrm(ks[18], (L, D_MODEL, D_FF), D_MODEL),
        'mlp_conv_w': nrm(ks[19], (L, MLP_CONV, D_FF), MLP_CONV),
        'mlp_conv_b': bias(ks[20], (L, D_FF)),
        'w_down': nrm(ks[21], (L, D_FF, D_MODEL), D_FF),
    }


def reference(x, attn_norm_w, w_in, lru_conv_w, lru_conv_b, lru_w_a, lru_b_a, lru_w_x, lru_b_x, lru_lambda,
              nsa_q_norm_w, nsa_k_norm_w, nsa_cmp_pos, nsa_cmp_w1, nsa_cmp_w2, w_out, mlp_norm_w,
              w_gate, w_up, mlp_conv_w, mlp_conv_b, w_down):
    offsets = np.cumsum(SPLIT_SIZES)[:-1].tolist()
    for l in range(DEPTH):
        hn = rms_norm(x, attn_norm_w[l])
        proj = jnp.einsum('bsd,de->bse', hn, w_in[l])
        (rq, rk, rv, rg, sq, sk, sv, lg, lr, nq, nkc, nvc, nks, nvs, nkw, nvw, ngate) = jnp.split(proj, offsets, axis=-1)
        y_ret = retention_mixer(rq, rk, rv, rg)
        y_sb = stick_breaking_mixer(sq, sk, sv)
        y_lru = rglru_mixer(lg, lr, lru_conv_w[l], lru_conv_b[l], lru_w_a[l], lru_b_a[l],
                            lru_w_x[l], lru_b_x[l], lru_lambda[l])
        y_nsa = nsa_mixer(nq, nkc, nvc, nks, nvs, nkw, nvw, ngate, nsa_q_norm_w[l], nsa_k_norm_w[l],
                          nsa_cmp_pos[l], nsa_cmp_w1[l], nsa_cmp_w2[l])
        mix = jnp.concatenate([y_ret, y_sb, y_lru, y_nsa], axis=-1).astype(x.dtype)
        x = x + jnp.einsum('bse,ed->bsd', mix, w_out[l]).astype(x.dtype)
        hn = rms_norm(x, mlp_norm_w[l])
        gt = causal_dwconv(jnp.einsum('bsd,df->bsf', hn, w_gate[l]), mlp_conv_w[l], mlp_conv_b[l])
        up = jnp.einsum('bsd,df->bsf', hn, w_up[l])
        x = x + jnp.einsum('bsf,fd->bsd', jax.nn.silu(gt) * up, w_down[l]).astype(x.dtype)
    return x
```

```python
import contextlib
import numpy as np
import concourse.bass as bass
import concourse.mybir as mybir
from concourse.bass_utils import run_bass_kernel_spmd

F32 = mybir.dt.float32
AF = mybir.ActivationFunctionType
ALU = mybir.AluOpType

S = 4096
DM = 4096
NIN = 11800
DFF = 11008
NFC = 86
EPS = 1e-6
BIG = 1.0e9
NEGM = 30000.0
GELU_C = 1.5957691216057308
FGROUPS = [(0, 15), (15, 30), (30, 44), (44, 58), (58, 72), (72, 86)]


class Res:
    __slots__ = ("lw", "rd")

    def __init__(self):
        self.lw = None
        self.rd = {}


class Op:
    __slots__ = ("eng", "fn", "deps", "idx", "signal", "stream", "sigcnt")


class Prog:
    ENGS = ("pe", "act", "dve", "pool", "sp")

    def __init__(self):
        self.ops = {e: [] for e in self.ENGS}
        self.stream_cnt = {}
        self.seen = {e: {} for e in self.ENGS}

    def _add_dep(self, op, tok):
        if tok is None:
            return
        if tok[0] == "E":
            _, e, idx = tok
            if e == op.eng and (e == "pe" or idx >= op.idx):
                return
            key, val = e, idx
        else:
            key = ("D", tok[1])
            val = self.stream_cnt[tok[1]]
        if val <= self.seen[op.eng].get(key, -1):
            return
        self.seen[op.eng][key] = val
        op.deps.append((key, val))

    def add(self, eng, fn, reads=(), writes=(), stream=None):
        op = Op()
        op.eng, op.fn, op.deps, op.idx, op.signal, op.stream = eng, fn, [], len(self.ops[eng]), False, stream
        for r in reads:
            self._add_dep(op, r.lw)
        for w in writes:
            self._add_dep(op, w.lw)
            for k, v in w.rd.items():
                self._add_dep(op, ("D", k[1], v) if isinstance(k, tuple) else ("E", k, v))
        if stream is not None:
            c = self.stream_cnt.get(stream, 0) + 1
            self.stream_cnt[stream] = c
            tok, rkey, rval = ("D", stream, c), ("D", stream), c
        else:
            tok, rkey, rval = ("E", eng, op.idx), eng, op.idx
        for r in reads:
            r.rd[rkey] = rval
        for w in writes:
            w.lw = tok
            w.rd = {}
        self.ops[eng].append(op)

    def barrier(self):
        last = {}
        for e in self.ENGS:
            j = len(self.ops[e]) - 1
            while j >= 0 and (self.ops[e][j].stream is not None or self.ops[e][j].fn is None):
                j -= 1
            last[e] = j
        for e in self.ENGS:
            op = Op()
            op.eng, op.fn, op.deps, op.idx, op.signal, op.stream = e, None, [], len(self.ops[e]), False, None
            for e2 in self.ENGS:
                if e2 != e and last[e2] >= 0:
                    self._add_dep(op, ("E", e2, last[e2]))
            for s, c in self.stream_cnt.items():
                self._add_dep(op, ("D", s, c))
            self.ops[e].append(op)

    def emit(self, nc):
        for e in self.ENGS:
            for op in self.ops[e]:
                for key, val in op.deps:
                    if not isinstance(key, tuple):
                        self.ops[key][val].signal = True
        for e in self.ENGS:
            c = 0
            for op in self.ops[e]:
                if op.signal:
                    c += 1
                op.sigcnt = c
        streams = sorted(self.stream_cnt.keys())
        with contextlib.ExitStack() as st:
            esem = {e: st.enter_context(nc.semaphore(f"s_{e}")) for e in self.ENGS}
            ssem = {s: st.enter_context(nc.semaphore(f"d_{s}")) for s in streams}
            block = st.enter_context(nc.Block())
            engobj = {"pe": "tensor", "act": "scalar", "dve": "vector", "pool": "gpsimd", "sp": "sync"}

            def make(e):
                def body(engine):
                    for op in self.ops[e]:
                        for key, val in op.deps:
                            if isinstance(key, tuple):
                                engine.wait_ge(ssem[key[1]], 16 * val)
                            else:
                                engine.wait_ge(esem[key], self.ops[key][val].sigcnt)
                        if op.fn is None:
                            continue
                        ins = op.fn(engine)
                        if op.stream is not None:
                            ins.then_inc(ssem[op.stream], 16)
                        elif op.signal:
                            ins.then_inc(esem[e], 1)
                    if e == "sp":
                        for s in streams:
                            engine.wait_ge(ssem[s], 16 * self.stream_cnt[s])
                return body

            for e in self.ENGS:
                getattr(block, engobj[e])(make(e))


class V:
    __slots__ = ("ap", "r")

    def __init__(self, ap, r=None):
        self.ap = ap
        self.r = r if r is not None else Res()

    def __getitem__(self, idx):
        return V(self.ap[idx], self.r)

    def re(self, pat, **kw):
        return V(self.ap.rearrange(pat, **kw), self.r)


class K:
    def __init__(self, nc, st):
        self.nc = nc
        self.st = st
        self.P = Prog()
        self.arena = st.enter_context(nc.sbuf_tensor("arena", [128, 47 * 1024], F32))
        self.off = 0
        self.psb = [V(st.enter_context(nc.psum_tensor(f"ps{i}", [128, 512], F32))[:]) for i in range(8)]
        self.nst = 0

    def reset(self, keep=0):
        self.P.barrier()
        self.off = keep

    def sb(self, *shape):
        n = int(np.prod(shape))
        ap = self.arena[:, self.off:self.off + n]
        self.off += n
        assert self.off <= 47 * 1024, self.off
        if len(shape) == 2:
            ap = ap.rearrange("p (a b) -> p a b", a=shape[0])
        elif len(shape) == 3:
            ap = ap.rearrange("p (a b c) -> p a b c", a=shape[0], b=shape[1])
        return V(ap)

    @staticmethod
    def _rs(vs):
        return [v.r for v in vs if isinstance(v, V)]

    def mm(self, ps, a, b, start=True, stop=True):
        self.P.add("pe", lambda e: e.matmul(ps.ap, lhsT=a.ap, rhs=b.ap, start=start, stop=stop), reads=[a.r, b.r], writes=[ps.r])

    def act(self, out, in_, func, bias=0.0, scale=1.0):
        b = bias.ap if isinstance(bias, V) else bias
        s = scale.ap if isinstance(scale, V) else scale
        self.P.add("act", lambda e: e.activation(out=out.ap, in_=in_.ap, func=func, bias=b, scale=s),
                   reads=self._rs([in_, bias, scale]), writes=[out.r])

    def tt(self, out, a, b, op, eng="dve"):
        self.P.add(eng, lambda e: e.tensor_tensor(out=out.ap, in0=a.ap, in1=b.ap, op=op), reads=[a.r, b.r], writes=[out.r])

    def ts(self, out, a, s1, op0, s2=None, op1=None, eng="dve"):
        x1 = s1.ap if isinstance(s1, V) else s1
        x2 = s2.ap if isinstance(s2, V) else s2
        if op1 is None:
            fn = lambda e: e.tensor_scalar(out=out.ap, in0=a.ap, scalar1=x1, scalar2=None, op0=op0)
        else:
            fn = lambda e: e.tensor_scalar(out=out.ap, in0=a.ap, scalar1=x1, scalar2=x2, op0=op0, op1=op1)
        self.P.add(eng, fn, reads=self._rs([a, s1, s2]), writes=[out.r])

    def stt(self, out, a, s, b, op0, op1):
        x = s.ap if isinstance(s, V) else s
        self.P.add("dve", lambda e: e.scalar_tensor_tensor(out=out.ap, in0=a.ap, scalar=x, in1=b.ap, op0=op0, op1=op1),
                   reads=self._rs([a, s, b]), writes=[out.r])

    def recip(self, out, a):
        self.P.add("dve", lambda e: e.reciprocal(out=out.ap, in_=a.ap), reads=[a.r], writes=[out.r])

    def copy(self, out, a, eng="dve"):
        if eng == "act":
            self.act(out, a, AF.Copy)
        else:
            self.P.add(eng, lambda e: e.tensor_copy(out=out.ap, in_=a.ap), reads=[a.r], writes=[out.r])

    def memset(self, out, val, eng="pool"):
        self.P.add(eng, lambda e: e.memset(out.ap, val), writes=[out.r])

    def dma(self, out, in_, stream=None):
        if stream is None:
            self.nst = (self.nst + 1) % 6
            stream = f"q{self.nst}"
        self.P.add("sp", lambda e: e.dma_start(out=out.ap, in_=in_.ap), reads=[in_.r], writes=[out.r], stream=stream)


def make_tables():
    f32 = np.float32
    T = {}
    t = np.arange(S, dtype=f32)
    j = np.arange(128)[:, None]
    inv = (1.0 / (10000.0 ** np.linspace(0.0, 1.0, 64, dtype=f32))).astype(f32)
    ang = (t[:, None] * inv[None, :]).astype(f32).astype(np.float64)
    c, s = np.cos(ang).T, np.sin(ang).T
    T["CR"] = np.concatenate([c, c], 0).astype(f32)
    T["SR"] = np.concatenate([-s, s], 0).astype(f32)
    invn = (500000.0 ** (-(np.arange(16, dtype=f32) / 16))).astype(f32)

    def nsa_tab(pos):
        a = (pos.astype(f32)[:, None] * invn[None, :]).astype(f32).astype(np.float64)
        c, s = np.cos(a).T, np.sin(a).T
        C = np.ones((128, len(pos)))
        Sn = np.zeros((128, len(pos)))
        C[0:16], C[16:32] = c, c
        Sn[0:16], Sn[16:32] = -s, s
        return C.astype(f32), Sn.astype(f32)

    T["CN"], T["SN"] = nsa_tab(np.arange(S))
    T["CNC"], T["SNC"] = nsa_tab(np.arange(256) * 16 + 31)
    u = np.arange(1024)[None, :]
    ER = np.zeros((128, 8, 1024), np.float64)
    for h in range(8):
        g = 1.0 - 2.0 ** (-5.0 - h)
        d = u - 384 - j
        ER[:, h, :] = np.where(d >= 0, g ** np.maximum(d, 0), 0.0) * (128.0 ** -0.5)
    T["ER"] = ER.astype(f32)
    u = np.arange(896)[None, :]
    T["MS"] = (u - 384 > j).astype(f32)
    T["MI"] = (u - 384 >= j).astype(f32)
    u = np.arange(1408)[None, :]
    T["WM"] = ((u - 384 - j >= 0) & (u - 384 - j < 512)).astype(f32)
    u = np.arange(3072)[None, :]
    T["TC"] = (u >= 31 + 16 * j).astype(f32)
    nci = np.arange(256)[:, None] * 16
    sj = np.arange(64)[None, :] * 64
    ov = np.maximum(0, np.minimum(nci + 32, sj + 64) - np.maximum(nci, sj)) / 16.0
    ov[255] = 0
    OVX = np.zeros((128, 2, 65), f32)
    for cc in range(2):
        OVX[:, cc, :64] = ov[cc * 128:(cc + 1) * 128]
        OVX[:, cc, 64] = 1.0
    T["OVX"] = OVX
    u = np.arange(126)[None, :]
    hi = (np.arange(128)[:, None] >= 64).astype(np.int64)
    TV = (u <= 62 + hi).astype(np.float64)
    TF = ((u == 62 + hi) | (u == 61 + hi)).astype(np.float64)
    T["TV"] = TV.astype(f32)
    T["TA"] = (BIG * TF + (TV - 1.0) * BIG).astype(f32)
    T["EX"] = (NEGM * (np.arange(S)[None, :] // 64 == np.arange(64)[:, None])).astype(f32)
    SG = np.zeros((24, 24, 128), f32)
    for i in range(24):
        SG[i, i, :] = 1.0
    T["SG"] = SG
    T["IDN"] = np.eye(128, dtype=f32)
    T["TRI"] = (np.arange(128)[:, None] > np.arange(128)[None, :]).astype(f32)
    return T


def swap16(a, axis):
    idx = np.arange(128)
    idx[0:16], idx[16:32] = np.arange(16, 32), np.arange(0, 16)
    return np.take(a, idx, axis=axis)


def layer_params(inp, l):
    f32 = np.float32
    pc = lambda v, n: np.ascontiguousarray(np.asarray(v, f32).reshape(n, 128).T)
    d = {}
    d["anw"] = pc(inp["attn_norm_w"][l], 32)
    d["mnw"] = pc(inp["mlp_norm_w"][l], 32)
    d["mcw"] = np.ascontiguousarray(np.asarray(inp["mlp_conv_w"][l], f32).reshape(3, NFC, 128).transpose(2, 1, 0))
    d["mcb"] = pc(inp["mlp_conv_b"][l], NFC)
    d["lcw"] = np.ascontiguousarray(np.asarray(inp["lru_conv_w"][l], f32).reshape(4, 8, 128).transpose(2, 1, 0))
    d["lcb"] = pc(inp["lru_conv_b"][l], 8)
    d["lba"] = pc(inp["lru_b_a"][l], 8)
    d["lbx"] = pc(inp["lru_b_x"][l], 8)
    d["llam"] = pc(inp["lru_lambda"][l], 8)
    d["lwa"] = np.ascontiguousarray(np.asarray(inp["lru_w_a"][l], f32).transpose(1, 0, 2))
    d["lwx"] = np.ascontiguousarray(np.asarray(inp["lru_w_x"][l], f32).transpose(1, 0, 2))
    qn = np.asarray(inp["nsa_q_norm_w"][l], f32)
    kn = np.asarray(inp["nsa_k_norm_w"][l], f32)
    nw = np.stack([qn, kn[0], kn[1], kn[2]], 1)
    d["nnw"] = np.ascontiguousarray(np.concatenate([nw, swap16(nw, 0)], 1))
    d["cpos"] = np.ascontiguousarray(np.asarray(inp["nsa_cmp_pos"][l], f32).transpose(2, 0, 1))
    d["cw1"] = np.ascontiguousarray(np.asarray(inp["nsa_cmp_w1"][l], f32).transpose(2, 0, 1, 3))
    w2 = np.asarray(inp["nsa_cmp_w2"][l], f32).reshape(2, 2, 128, 128)
    w2 = np.concatenate([w2, swap16(w2[0:1], 3)], 0)
    d["cw2"] = np.ascontiguousarray(w2.transpose(2, 0, 1, 3))
    d["w_in"] = np.asarray(inp["w_in"][l], f32)
    d["w_out"] = np.asarray(inp["w_out"][l], f32)
    d["w_gate"] = np.asarray(inp["w_gate"][l], f32)
    d["w_up"] = np.asarray(inp["w_up"][l], f32)
    d["w_down"] = np.asarray(inp["w_down"][l], f32)
    return d


TAB_SHAPES = None
PARAM_SHAPES = {
    "anw": [128, 32], "mnw": [128, 32], "mcw": [128, NFC, 3], "mcb": [128, NFC], "lcw": [128, 8, 4], "lcb": [128, 8],
    "lba": [128, 8], "lbx": [128, 8], "llam": [128, 8], "lwa": [128, 8, 128], "lwx": [128, 8, 128], "nnw": [128, 8],
    "cpos": [128, 2, 32], "cw1": [128, 2, 32, 256], "cw2": [128, 3, 2, 128],
    "w_in": [DM, NIN], "w_out": [DM, DM], "w_gate": [DM, DFF], "w_up": [DM, DFF], "w_down": [DFF, DM],
}


def build_layer(tables, phases=("A", "RET", "SB", "LRU", "NSA", "C", "D"), dbg_in=(), dbg_out=(), ntb=8):
    nc = bass.Bass("TRN2", target_bir_lowering=False)
    D = {}

    def dram(name, shape, kind="Internal"):
        if name in dbg_in:
            kind = "ExternalInput"
        if name in dbg_out:
            kind = "ExternalOutput"
        D[name] = V(nc.dram_tensor(name, list(shape), F32, kind=kind).ap())
        return D[name]

    dram("x0", [128, 32, S], "ExternalInput")
    dram("y", [128, 32, S], "ExternalOutput")
    need = {"w_in": "A", "w_out": "C", "w_gate": "D", "w_up": "D", "w_down": "D"}
    for n, shp in PARAM_SHAPES.items():
        if n in need and need[n] not in phases:
            shp = [128, 128]
        dram(n, shp, "ExternalInput")
    for n, a in tables.items():
        dram(n, a.shape, "ExternalInput")
    dram("PF", [NIN, S])
    dram("PT", [S, 2560])
    dram("MIX", [128, 32, S])

    with contextlib.ExitStack() as st:
        k = K(nc, st)
        PS = k.psb
        ones = k.sb(128)
        k.memset(ones, 1.0)
        keep0 = k.off

        def rmsnorm_block(xb, nw, scr, n=512):
            sq = [scr[0], scr[1]]
            rstd = scr[2]
            for c in range(32):
                k.act(sq[c % 2], xb[:, c, :], AF.Square)
                k.mm(PS[7][:, 0:n], ones, sq[c % 2], start=(c == 0), stop=(c == 31))
            k.act(rstd, PS[7][:, 0:n], AF.Sqrt, bias=EPS, scale=1.0 / DM)
            k.recip(rstd, rstd)
            for c in range(32):
                k.stt(xb[:, c, :], xb[:, c, :], nw[:, c:c + 1], rstd, ALU.mult, ALU.mult)

        if "A" in phases:
            k.reset(keep0)
            nw = k.sb(32)
            k.dma(nw, D["anw"])
            xb = k.sb(32, 512)
            nscr = (k.sb(512), k.sb(512), k.sb(512))
            wt = [k.sb(32, 256), k.sb(32, 256)]
            ev = [k.sb(512), k.sb(512)]
            win = D["w_in"].re("(c p) f -> p c f", p=128)
            TOKMAJ = {8: 0, 9: 256, 10: 512, 11: 768, 24: 1024, 25: 1280, 26: 1536, 27: 1792, 43: 2048, 45: 2304}
            nblk = 47
            for tb in range(ntb):
                ts_ = slice(tb * 512, (tb + 1) * 512)
                k.dma(xb, D["x0"][:, :, ts_])
                rmsnorm_block(xb, nw, nscr)
                ei = 0
                k.dma(wt[0], win[:, :, 0:256], "w0")
                for wb in range(nblk):
                    s = wb % 2
                    if wb + 1 < nblk:
                        f1 = (wb + 1) * 256
                        wd = 256 if wb + 1 < 46 else 24
                        k.dma(wt[1 - s][:, :, 0:wd], win[:, :, f1:f1 + wd], f"w{1 - s}")
                    f0 = wb * 256
                    if wb == 46:
                        ps = PS[ei % 4]
                        for c in range(32):
                            k.mm(ps[0:24, :], wt[s][:, c, 0:24], xb[:, c, :], start=(c == 0), stop=(c == 31))
                        k.copy(ev[ei % 2][0:24, :], ps[0:24, :], "act")
                        k.dma(D["PF"][f0:f0 + 24, ts_], ev[ei % 2][0:24, :])
                        ei += 1
                    elif wb in TOKMAJ:
                        for tt in range(4):
                            ps = PS[ei % 4]
                            for c in range(32):
                                k.mm(ps[:, 0:256], xb[:, c, tt * 128:(tt + 1) * 128], wt[s][:, c, :], start=(c == 0), stop=(c == 31))
                            k.copy(ev[ei % 2][:, 0:256], ps[:, 0:256], "act" if ei % 2 else "dve")
                            t0 = tb * 512 + tt * 128
                            k.dma(D["PT"][t0:t0 + 128, TOKMAJ[wb]:TOKMAJ[wb] + 256], ev[ei % 2][:, 0:256])
                            ei += 1
                    else:
                        for sub in range(2):
                            ps = PS[ei % 4]
                            for c in range(32):
                                k.mm(ps, wt[s][:, c, sub * 128:(sub + 1) * 128], xb[:, c, :], start=(c == 0), stop=(c == 31))
                            k.copy(ev[ei % 2], ps, "act" if ei % 2 else "dve")
                            k.dma(D["PF"][f0 + sub * 128:f0 + (sub + 1) * 128, ts_], ev[ei % 2])
                            ei += 1

        if "RET" in phases:
            k.reset(keep0)
            crp, srp = [k.sb(512), k.sb(512)], [k.sb(512), k.sb(512)]
            ER = k.sb(8, 1024)
            k.dma(ER, D["ER"])
            Q, Kt, G = k.sb(S), k.sb(S), k.sb(S)
            Vt = k.sb(32, 128)
            xa, xs_ = [k.sb(512), k.sb(512)], [k.sb(512), k.sb(512)]
            ta, tb_ = [k.sb(512), k.sb(512)], [k.sb(512), k.sb(512)]
            Pt = [k.sb(512), k.sb(512)]
            osb, sq, rstd = k.sb(512), k.sb(512), k.sb(512)
            for h in range(8):
                i = 0
                for (dst, row) in ((Q, h * 128), (Kt, 1024 + h * 128)):
                    for pc in range(8):
                        cs = slice(pc * 512, (pc + 1) * 512)
                        b = i % 2
                        k.dma(xa[b], D["PF"][row:row + 128, cs])
                        k.dma(xs_[b][0:64, :], D["PF"][row + 64:row + 128, cs])
                        k.dma(xs_[b][64:128, :], D["PF"][row:row + 64, cs])
                        k.dma(crp[b], D["CR"][:, cs])
                        k.dma(srp[b], D["SR"][:, cs])
                        k.tt(ta[b], xa[b], crp[b], ALU.mult)
                        k.tt(tb_[b], xs_[b], srp[b], ALU.mult, eng="pool")
                        k.tt(dst[:, cs], ta[b], tb_[b], ALU.add)
                        i += 1
                k.dma(G, D["PF"][3072 + h * 128:3072 + (h + 1) * 128, :])
                k.act(G, G, AF.Silu)
                k.dma(Vt, D["PT"][:, h * 128:(h + 1) * 128].re("(kb p) d -> p kb d", p=128))
                g = 1.0 - 2.0 ** (-5.0 - h)
                for qb in range(8):
                    qs = slice(qb * 512, (qb + 1) * 512)
                    nk = 4 * qb + 4
                    for kb in range(nk):
                        ps = PS[kb % 2]
                        k.mm(ps, Kt[:, kb * 128:(kb + 1) * 128], Q[:, qs])
                        r = kb - 4 * qb
                        if r >= 0:
                            sc, u0 = 1.0, 384 - 128 * r
                        else:
                            sc, u0 = float(g ** (512 * qb - 128 * kb - 128)), 512
                        k.stt(Pt[kb % 2], ps, sc, ER[:, h, u0:u0 + 512], ALU.mult, ALU.mult)
                        k.mm(PS[2], Vt[:, kb, :], Pt[kb % 2], start=(kb == 0), stop=(kb == nk - 1))
                    k.copy(osb, PS[2], "act")
                    k.act(sq, PS[2], AF.Square)
                    k.mm(PS[3], ones, sq)
                    k.act(rstd, PS[3], AF.Sqrt, bias=EPS, scale=1.0 / 128)
                    k.recip(rstd, rstd)
                    k.tt(osb, osb, rstd, ALU.mult)
                    k.tt(osb, osb, G[:, qs], ALU.mult)
                    k.dma(D["MIX"][:, h, qs], osb)

        if "SB" in phases:
            k.reset(keep0)
            tri = k.sb(128)
            k.dma(tri, D["TRI"])
            MS = k.sb(896)
            k.dma(MS, D["MS"])
            Q, Kt = k.sb(S), k.sb(S)
            Vt = k.sb(32, 128)
            Acc = k.sb(512)
            E_, L_, t1, w_ = ([k.sb(512), k.sb(512)] for _ in range(4))
            osb = k.sb(512)
            scale = 128.0 ** -0.5
            for h in range(8):
                k.dma(Q, D["PF"][4096 + h * 128:4096 + (h + 1) * 128, :])
                k.dma(Kt, D["PF"][5120 + h * 128:5120 + (h + 1) * 128, :])
                k.dma(Vt, D["PT"][:, 1024 + h * 128:1024 + (h + 1) * 128].re("(kb p) d -> p kb d", p=128))
                for qb in range(8):
                    qs = slice(qb * 512, (qb + 1) * 512)
                    nk = 4 * qb + 4
                    k.memset(Acc, 0.0)
                    for i, kb in enumerate(range(nk - 1, -1, -1)):
                        b = i % 2
                        r = kb - 4 * qb
                        pz, pc_ = PS[b], PS[2 + b]
                        k.mm(pz, Kt[:, kb * 128:(kb + 1) * 128], Q[:, qs])
                        k.act(E_[b], pz, AF.Exp, scale=scale)
                        k.act(L_[b], E_[b], AF.Ln, bias=1.0)
                        if r >= 0:
                            msk = MS[:, 384 - 128 * r:896 - 128 * r]
                            k.tt(L_[b], L_[b], msk, ALU.mult)
                        k.mm(pc_, tri, L_[b], start=True, stop=(i == 0))
                        if i > 0:
                            k.mm(pc_, ones, Acc, start=False, stop=True)
                        k.stt(t1[b], pz, scale, L_[b], ALU.mult, ALU.subtract)
                        k.tt(t1[b], t1[b], pc_, ALU.subtract)
                        k.act(w_[b], t1[b], AF.Exp)
                        if r >= 0:
                            k.tt(w_[b], w_[b], msk, ALU.mult, eng="pool")
                        k.tt(Acc, Acc, L_[b], ALU.add, eng="pool")
                        k.mm(PS[4], Vt[:, kb, :], w_[b], start=(i == 0), stop=(i == nk - 1))
                    k.copy(osb, PS[4], "act")
                    k.dma(D["MIX"][:, 8 + h, qs], osb)

        if "LRU" in phases:
            k.reset(keep0)
            lcw, lcb, lba, lbx, llam = k.sb(8, 4), k.sb(8), k.sb(8), k.sb(8), k.sb(8)
            for a, n in ((lcw, "lcw"), (lcb, "lcb"), (lba, "lba"), (lbx, "lbx"), (llam, "llam")):
                k.dma(a, D[n])
            WA, WX = k.sb(8, 128), k.sb(8, 128)
            k.dma(WA, D["lwa"])
            k.dma(WX, D["lwx"])
            c8 = k.sb(8)
            k.act(c8, llam, AF.Exp, scale=-1.0)
            k.act(c8, c8, AF.Ln, bias=1.0)
            k.ts(c8, c8, -8.0, ALU.mult)
            XP, xr, R, I_, A_, G, T2 = k.sb(S + 3), k.sb(S), k.sb(S), k.sb(S), k.sb(S), k.sb(S), k.sb(S)
            for n in range(8):
                k.memset(XP[:, 0:3], 0.0)
                k.dma(XP[:, 3:S + 3], D["PF"][8192 + n * 128:8192 + (n + 1) * 128, :])
                k.dma(G, D["PF"][7168 + n * 128:7168 + (n + 1) * 128, :])
                k.ts(xr, XP[:, 0:S], lcw[:, n, 0:1], ALU.mult, lcb[:, n:n + 1], ALU.add)
                for j in range(1, 4):
                    k.stt(xr, XP[:, j:S + j], lcw[:, n, j:j + 1], xr, ALU.mult, ALU.add)
                for pc in range(8):
                    cs = slice(pc * 512, (pc + 1) * 512)
                    k.mm(PS[0], WA[:, n, :], xr[:, cs])
                    k.act(R[:, cs], PS[0], AF.Sigmoid, bias=lba[:, n:n + 1])
                    k.mm(PS[1], WX[:, n, :], xr[:, cs])
                    k.act(I_[:, cs], PS[1], AF.Sigmoid, bias=lbx[:, n:n + 1])
                k.act(A_, R, AF.Exp, scale=c8[:, n:n + 1])
                k.tt(T2, A_, A_, ALU.mult)
                k.act(T2, T2, AF.Sqrt, bias=1.0, scale=-1.0)
                k.memset(T2[:, 0:1], 1.0, eng="dve")
                k.tt(I_, I_, xr, ALU.mult)
                k.tt(I_, I_, T2, ALU.mult)
                k.P.add("dve", lambda e, o=R, a=A_, u=I_: e.tensor_tensor_scan(out=o.ap, data0=a.ap, data1=u.ap, initial=0.0,
                                                                                 op0=ALU.mult, op1=ALU.add),
                        reads=[A_.r, I_.r], writes=[R.r])
                k.tt(T2, G, G, ALU.mult)
                k.ts(T2, T2, 0.044715, ALU.mult, 1.0, ALU.add)
                k.tt(T2, T2, G, ALU.mult)
                k.act(T2, T2, AF.Sigmoid, scale=GELU_C)
                k.tt(T2, T2, G, ALU.mult)
                k.tt(T2, T2, R, ALU.mult)
                k.dma(D["MIX"][:, 16 + n, :], T2)

        if "NSA" in phases:
            k.reset(keep0)
            nnw = k.sb(8)
            k.dma(nnw, D["nnw"])
            CN, SN = D["CN"], D["SN"]
            GS = k.sb(S)
            k.dma(GS[0:24, :], D["PF"][11776:11800, :])
            k.act(GS[0:24, :], GS[0:24, :], AF.Sigmoid)
            SG = k.sb(24, 128)
            k.dma(SG[0:24], D["SG"])
            EX = k.sb(S)
            k.dma(EX[0:64, :], D["EX"])
            idn = k.sb(128)
            k.dma(idn, D["IDN"])
            MI, WM, TC = k.sb(896), k.sb(1408), k.sb(3072)
            k.dma(MI, D["MI"])
            k.dma(WM, D["WM"])
            k.dma(TC, D["TC"])
            OVX = k.sb(2, 65)
            k.dma(OVX, D["OVX"])
            TV, TA = k.sb(126), k.sb(126)
            k.dma(TV, D["TV"])
            k.dma(TA, D["TA"])
            keepn = k.off
            qscale = 128.0 ** -0.5

            def norm_rot(dst, row, cs, wi, C, Sn, scale, tmp):
                xa, xs_, sq, rstd, ta, cpz, spz = tmp
                k.dma(cpz, C)
                k.dma(spz, Sn)
                C, Sn = cpz, spz
                k.dma(xa, D["PF"][row:row + 128, cs])
                k.dma(xs_[0:16, :], D["PF"][row + 16:row + 32, cs])
                k.dma(xs_[16:32, :], D["PF"][row:row + 16, cs])
                k.act(sq, xa, AF.Square)
                k.mm(PS[7], ones, sq)
                k.act(rstd, PS[7], AF.Sqrt, bias=EPS / (scale * scale), scale=1.0 / (128.0 * scale * scale))
                k.recip(rstd, rstd)
                k.stt(ta, xa, nnw[:, wi:wi + 1], C, ALU.mult, ALU.mult)
                k.stt(xs_, xs_, nnw[:, 4 + wi:5 + wi], Sn, ALU.mult, ALU.mult)
                k.tt(ta, ta, xs_, ALU.add, eng="pool")
                k.tt(dst, ta, rstd, ALU.mult)

            for g in range(2):
                k.reset(keepn)
                KC = k.sb(256)
                VC = k.sb(2, 128)
                keepc = k.off
                cT = k.sb(S + 32)
                w1 = k.sb(32, 256)
                cpos = k.sb(2, 32)
                k.dma(cpos, D["cpos"])
                w2 = k.sb(3, 2, 128)
                k.dma(w2, D["cw2"])
                CNC, SNC = k.sb(256), k.sb(256)
                k.dma(CNC, D["CNC"])
                k.dma(SNC, D["SNC"])
                tmpb = [k.sb(256), k.sb(256)]
                hid = k.sb(2, 256)
                hx, h2 = k.sb(256), k.sb(256)
                for a in range(2):
                    row = (10240 if a == 0 else 10496) + g * 128
                    k.memset(cT[:, S:S + 32], 0.0)
                    k.dma(cT[:, 0:S], D["PF"][row:row + 128, :])
                    k.dma(w1, D["cw1"][:, a])
                    v0 = cT[:, 0:S].re("p (n s) -> p n s", s=16)
                    v1 = cT[:, 16:S + 16].re("p (n s) -> p n s", s=16)
                    for l in range(32):
                        src = v0[:, :, l] if l < 16 else v1[:, :, l - 16]
                        k.ts(tmpb[l % 2], src, cpos[:, a, l:l + 1], ALU.add)
                        for hc in range(2):
                            k.mm(PS[hc][:, 0:256], w1[:, l, hc * 128:(hc + 1) * 128], tmpb[l % 2], start=(l == 0), stop=(l == 31))
                    for hc in range(2):
                        k.copy(hx, PS[hc][:, 0:256], "act")
                        k.tt(h2, hx, hx, ALU.mult)
                        k.ts(h2, h2, 0.044715, ALU.mult, 1.0, ALU.add)
                        k.tt(h2, h2, hx, ALU.mult)
                        k.act(h2, h2, AF.Sigmoid, scale=GELU_C)
                        k.tt(hid[:, hc, :], h2, hx, ALU.mult)
                    if a == 0:
                        for hc in range(2):
                            k.mm(PS[2][:, 0:256], w2[:, 0, hc, :], hid[:, hc, :], start=(hc == 0), stop=(hc == 1))
                            k.mm(PS[3][:, 0:256], w2[:, 2, hc, :], hid[:, hc, :], start=(hc == 0), stop=(hc == 1))
                        k.copy(hx, PS[2][:, 0:256], "act")
                        k.act(h2, PS[2][:, 0:256], AF.Square)
                        k.mm(PS[7][:, 0:256], ones, h2)
                        k.act(h2, PS[7][:, 0:256], AF.Sqrt, bias=EPS, scale=1.0 / 128)
                        k.recip(h2, h2)
                        k.stt(hx, hx, nnw[:, 1:2], CNC, ALU.mult, ALU.mult)
                        k.stt(tmpb[0], PS[3][:, 0:256], nnw[:, 5:6], SNC, ALU.mult, ALU.mult)
                        k.tt(hx, hx, tmpb[0], ALU.add)
                        k.tt(KC, hx, h2, ALU.mult)
                    else:
                        for ncn in range(2):
                            for hc in range(2):
                                k.mm(PS[2][:, 0:128], hid[:, hc, ncn * 128:(ncn + 1) * 128], w2[:, 1, hc, :], start=(hc == 0), stop=(hc == 1))
                            k.copy(VC[:, ncn, :], PS[2][:, 0:128], "act")
                k.reset(keepc)
                KS, KW = k.sb(S), k.sb(S)
                VS, VW = k.sb(32, 128), k.sb(32, 128)
                tmp = tuple(k.sb(512) for _ in range(7))
                k.memset(tmp[1], 0.0)
                for (dst, row, wi) in ((KS, 10752 + g * 128, 2), (KW, 11264 + g * 128, 3)):
                    for pc in range(8):
                        cs = slice(pc * 512, (pc + 1) * 512)
                        norm_rot(dst[:, cs], row, cs, wi, CN[:, cs], SN[:, cs], 1.0, tmp)
                k.dma(VS, D["PT"][:, 2048 + g * 128:2048 + (g + 1) * 128].re("(kb p) d -> p kb d", p=128))
                k.dma(VW, D["PT"][:, 2304 + g * 128:2304 + (g + 1) * 128].re("(kb p) d -> p kb d", p=128))
                Qr = k.sb(4, 512)
                OUT = k.sb(4, 512)
                PSL = k.sb(4, 64)
                SELB = k.sb(512)
                ec = [k.sb(512), k.sb(512)]
                et = [k.sb(512), k.sb(512)]
                rz, t3 = k.sb(512), k.sb(512)
                rzc = k.sb(4)
                score, sc2, selm = k.sb(64), k.sb(64), k.sb(64)
                m1, m2 = k.sb(8), k.sb(8)

                def finish_branch(r, hq, br, po, pz, first):
                    k.mm(PS[6], SG[0:24, hq * 3 + br, :], GS[0:24, qs])
                    k.ts(rz, pz, 1e-30, ALU.max)
                    k.recip(rz, rz)
                    k.tt(rz, rz, PS[6], ALU.mult)
                    if first:
                        k.tt(OUT[:, r, :], rz, po, ALU.mult)
                    else:
                        k.tt(t3, rz, po, ALU.mult)
                        k.tt(OUT[:, r, :], OUT[:, r, :], t3, ALU.add, eng="pool")

                for qb in range(8):
                    qs = slice(qb * 512, (qb + 1) * 512)
                    k.memset(PSL, 0.0)
                    ncc = 2 if qb >= 4 else 1
                    for r in range(4):
                        hq = 4 * g + r
                        norm_rot(Qr[:, r, :], 9216 + hq * 128, qs, 0, CN[:, qs], SN[:, qs], qscale, tmp)
                        for c in range(ncc):
                            k.mm(PS[c], KC[:, c * 128:(c + 1) * 128], Qr[:, r, :])
                            k.act(ec[c], PS[c], AF.Exp)
                            u0 = min(512 * qb - 2048 * c, 2560)
                            k.tt(ec[c], ec[c], TC[:, u0:u0 + 512], ALU.mult)
                        for c in range(ncc):
                            k.mm(PS[2], VC[:, c, :], ec[c], start=(c == 0), stop=(c == ncc - 1))
                        for c in range(ncc):
                            k.mm(PS[3], ones, ec[c], start=(c == 0), stop=(c == ncc - 1))
                        for tt in range(4):
                            for c in range(ncc):
                                k.mm(PS[4][:, tt * 65:tt * 65 + 65], ec[c][:, tt * 128:(tt + 1) * 128], OVX[:, c, :], start=(c == 0), stop=(c == ncc - 1))
                        finish_branch(r, hq, 0, PS[2], PS[3], True)
                        k.ts(rzc, PS[4][:, 0:260].re("p (t j) -> p t j", j=65)[:, :, 64], 1e-30, ALU.max)
                        k.recip(rzc, rzc)
                        for tt in range(4):
                            k.stt(PSL[:, tt, :], PS[4][:, tt * 65:tt * 65 + 64], rzc[:, tt:tt + 1], PSL[:, tt, :], ALU.mult, ALU.add)
                    for tt in range(4):
                        s0 = 62 - 2 * (4 * qb + tt)
                        k.tt(score, PSL[:, tt, :], TV[:, s0:s0 + 64], ALU.mult)
                        k.tt(score, score, TA[:, s0:s0 + 64], ALU.add)
                        k.memset(score[:, 0:1], BIG, eng="dve")
                        k.P.add("dve", lambda e: e.max(out=m1.ap, in_=score.ap), reads=[score.r], writes=[m1.r])
                        k.P.add("dve", lambda e: e.match_replace(out=sc2.ap, in_to_replace=m1.ap, in_values=score.ap, imm_value=-3.0e38),
                                reads=[score.r, m1.r], writes=[sc2.r])
                        k.P.add("dve", lambda e: e.max(out=m2.ap, in_=sc2.ap), reads=[sc2.r], writes=[m2.r])
                        k.ts(selm, score, m2[:, 7:8], ALU.is_ge, 1.0, ALU.subtract)
                        k.mm(PS[5][0:64, 0:128], selm, idn)
                        k.copy(SELB[0:64, tt * 128:(tt + 1) * 128], PS[5][0:64, 0:128], "act")
                    for r in range(4):
                        hq = 4 * g + r
                        nk = 4 * qb + 4
                        for kb in range(nk):
                            b = kb % 2
                            k.mm(PS[b], KS[:, kb * 128:(kb + 1) * 128], Qr[:, r, :], start=True, stop=False)
                            k.mm(PS[b], EX[0:64, kb * 128:(kb + 1) * 128], SELB[0:64, :], start=False, stop=True)
                            k.act(et[b], PS[b], AF.Exp)
                            rr = kb - 4 * qb
                            if rr >= 0:
                                k.tt(et[b], et[b], MI[:, 384 - 128 * rr:896 - 128 * rr], ALU.mult)
                            k.mm(PS[2], VS[:, kb, :], et[b], start=(kb == 0), stop=(kb == nk - 1))
                            k.mm(PS[3], ones, et[b], start=(kb == 0), stop=(kb == nk - 1))
                        finish_branch(r, hq, 1, PS[2], PS[3], False)
                        k0 = max(0, 4 * qb - 4)
                        for kb in range(k0, nk):
                            b = kb % 2
                            k.mm(PS[b], KW[:, kb * 128:(kb + 1) * 128], Qr[:, r, :])
                            k.act(et[b], PS[b], AF.Exp)
                            dl = 512 * qb - 128 * kb
                            k.tt(et[b], et[b], WM[:, dl + 384:dl + 896], ALU.mult)
                            k.mm(PS[2], VW[:, kb, :], et[b], start=(kb == k0), stop=(kb == nk - 1))
                            k.mm(PS[3], ones, et[b], start=(kb == k0), stop=(kb == nk - 1))
                        finish_branch(r, hq, 2, PS[2], PS[3], False)
                        k.dma(D["MIX"][:, 24 + hq, qs], OUT[:, r, :])

        if "C" in phases:
            k.reset(keep0)
            mb = k.sb(32, 512)
            wt = [k.sb(32, 256), k.sb(32, 256)]
            xr_ = [k.sb(512), k.sb(512)]
            wout = D["w_out"].re("(c p) f -> p c f", p=128)
            for tb in range(ntb):
                ts_ = slice(tb * 512, (tb + 1) * 512)
                k.dma(mb, D["MIX"][:, :, ts_])
                k.dma(wt[0], wout[:, :, 0:256], "w0")
                ei = 0
                for wb in range(16):
                    s = wb % 2
                    if wb + 1 < 16:
                        k.dma(wt[1 - s], wout[:, :, (wb + 1) * 256:(wb + 2) * 256], f"w{1 - s}")
                    for sub in range(2):
                        dc = wb * 2 + sub
                        ps = PS[ei % 4]
                        k.dma(xr_[ei % 2], D["x0"][:, dc, ts_])
                        for c in range(32):
                            k.mm(ps, wt[s][:, c, sub * 128:(sub + 1) * 128], mb[:, c, :], start=(c == 0), stop=(c == 31))
                        k.tt(xr_[ei % 2], xr_[ei % 2], ps, ALU.add)
                        k.dma(D["y"][:, dc, ts_], xr_[ei % 2])
                        ei += 1

        if "D" in phases:
            k.reset(keep0)
            nw = k.sb(32)
            k.dma(nw, D["mnw"])
            mcw, mcb = k.sb(NFC, 3), k.sb(NFC)
            k.dma(mcw, D["mcw"])
            k.dma(mcb, D["mcb"])
            gprev = k.sb(NFC, 2)
            k.memset(gprev, 0.0)
            xb = k.sb(32, 512)
            nscr = (k.sb(512), k.sb(512), k.sb(512))
            hT = k.sb(15, 512)
            wgu = [k.sb(32, 128) for _ in range(3)]
            wd = [k.sb(15, 128), k.sb(15, 128)]
            gx = [k.sb(514), k.sb(514)]
            acc = [k.sb(512), k.sb(512)]
            xr_ = [k.sb(512), k.sb(512)]
            wg_ = D["w_gate"].re("(c p) f -> p c f", p=128)
            wu_ = D["w_up"].re("(c p) f -> p c f", p=128)
            wdn = D["w_down"].re("(fc p) d -> p fc d", p=128)
            yres = D["y"]
            for tb in range(ntb):
                ts_ = slice(tb * 512, (tb + 1) * 512)
                k.dma(xb, yres[:, :, ts_])
                rmsnorm_block(xb, nw, nscr)
                wi = 0
                for (f0, f1) in FGROUPS:
                    for fc in range(f0, f1):
                        fs = slice(fc * 128, (fc + 1) * 128)
                        sg, su = wgu[wi % 3], wgu[(wi + 1) % 3]
                        k.dma(sg, wg_[:, :, fs])
                        k.dma(su, wu_[:, :, fs])
                        wi += 2
                        b = fc % 2
                        for c in range(32):
                            k.mm(PS[b], sg[:, c, :], xb[:, c, :], start=(c == 0), stop=(c == 31))
                        for c in range(32):
                            k.mm(PS[2 + b], su[:, c, :], xb[:, c, :], start=(c == 0), stop=(c == 31))
                        k.copy(gx[b][:, 0:2], gprev[:, fc, :], "pool")
                        k.copy(gx[b][:, 2:514], PS[b], "act")
                        k.copy(gprev[:, fc, :], gx[b][:, 512:514], "pool")
                        k.ts(acc[b], gx[b][:, 0:512], mcw[:, fc, 0:1], ALU.mult, mcb[:, fc:fc + 1], ALU.add)
                        k.stt(acc[b], gx[b][:, 1:513], mcw[:, fc, 1:2], acc[b], ALU.mult, ALU.add)
                        k.stt(acc[b], gx[b][:, 2:514], mcw[:, fc, 2:3], acc[b], ALU.mult, ALU.add)
                        k.act(acc[b], acc[b], AF.Silu)
                        k.tt(hT[:, fc - f0, :], acc[b], PS[2 + b], ALU.mult)
                    nf = f1 - f0
                    for db in range(32):
                        s = db % 2
                        k.dma(wd[s][:, 0:nf, :], wdn[:, f0:f1, db * 128:(db + 1) * 128])
                        k.dma(xr_[s], yres[:, db, ts_], "yl")
                        ps = PS[4 + s]
                        for j in range(nf):
                            k.mm(ps, wd[s][:, j, :], hT[:, j, :], start=(j == 0), stop=(j == nf - 1))
                        k.tt(xr_[s], xr_[s], ps, ALU.add)
                        k.dma(yres[:, db, ts_], xr_[s], "ys")
        k.P.barrier()
        k.P.emit(nc)
    return nc


_CACHE = {}


def kernel(**inputs):
    tables = make_tables()
    if "nc" not in _CACHE:
        _CACHE["nc"] = build_layer(tables)
    nc = _CACHE["nc"]
    x = np.asarray(inputs["x"], np.float32)
    B = x.shape[0]
    cur = [np.ascontiguousarray(x[b].T.reshape(32, 128, S).transpose(1, 0, 2)) for b in range(B)]
    for l in range(4):
        lp = layer_params(inputs, l)
        in_maps = []
        for b in range(B):
            m = {"x0": cur[b]}
            m.update(lp)
            m.update(tables)
            in_maps.append(m)
        res = run_bass_kernel_spmd(nc, in_maps, core_ids=list(range(B)))
        cur = [np.ascontiguousarray(res.results[b]["y"]) for b in range(B)]
    out = np.stack([cur[b].transpose(1, 0, 2).reshape(DM, S).T for b in range(B)], 0)
    return np.ascontiguousarray(out.astype(np.float32))
```
